# Optimizing a Trainium2 kernel written in Bass

```python
import jax, jax.numpy as jnp
from jax import lax
import numpy as np

D_MODEL = 2048
BATCH = 4
SEQ = 2048
DEPTH = 4

D_FF = 5632
SSM_WIDTH = 1024
SSM_GROUP = 16
SSM_GROUPS = SSM_WIDTH // SSM_GROUP
SSM_STATE = 64
DT_MIN = 1e-3
DT_MAX = 1e-1
GDN_HEADS = 8
GDN_HEAD_DIM = 128
GDN_WIDTH = GDN_HEADS * GDN_HEAD_DIM
CONV_K = 4
CHUNK = 64
IN_SIZES = (SSM_WIDTH, GDN_WIDTH, GDN_WIDTH, GDN_WIDTH, GDN_WIDTH, GDN_HEADS, GDN_HEADS, D_MODEL, D_MODEL)
IN_COLS = sum(IN_SIZES)
LN_EPS = 1e-5
RMS_EPS = 1e-6
L2_EPS = 1e-6

kernel_name = 'hybrid_s5_gdn_macaron_deepnorm'


def layer_norm(x, g, b):
    xf = x.astype(jnp.float32)
    mu = jnp.mean(xf, axis=-1, keepdims=True)
    var = jnp.mean(jnp.square(xf - mu), axis=-1, keepdims=True)
    y = (xf - mu) * lax.rsqrt(var + LN_EPS) * g.astype(jnp.float32) + b.astype(jnp.float32)
    return y.astype(x.dtype)


def swiglu_ffn(x, w_gu, w_down):
    gate, up = jnp.split(x @ w_gu, 2, axis=-1)
    return (jax.nn.silu(gate) * up) @ w_down


def cmul(ar, ai, br, bi):
    return ar * br - ai * bi, ar * bi + ai * br


def s5_branch(u, a_re, a_im, log_dt, b_re, b_im, c_re, c_im, d_skip, glu_w, glu_b):
    f32 = jnp.float32
    bsz, seq, _ = u.shape
    ug = u.astype(f32).reshape(bsz, seq, SSM_GROUPS, SSM_GROUP)
    dt = jnp.exp(log_dt.astype(f32))[:, None]
    lr, li = a_re.astype(f32), a_im.astype(f32)
    mag = jnp.exp(lr * dt)
    lbar_r, lbar_i = mag * jnp.cos(li * dt), mag * jnp.sin(li * dt)
    den = lr * lr + li * li
    zr, zi = cmul(lbar_r - 1.0, lbar_i, lr / den, -li / den)
    bbar_r, bbar_i = cmul(zr[:, :, None], zi[:, :, None], b_re.astype(f32), b_im.astype(f32))
    bu_r = jnp.einsum('blgh,gph->blgp', ug, bbar_r)
    bu_i = jnp.einsum('blgh,gph->blgp', ug, bbar_i)
    a_r = jnp.broadcast_to(lbar_r, (1, seq) + lbar_r.shape)
    a_i = jnp.broadcast_to(lbar_i, (1, seq) + lbar_i.shape)

    def combine(e_early, e_late):
        a1r, a1i, b1r, b1i = e_early
        a2r, a2i, b2r, b2i = e_late
        ar, ai = cmul(a2r, a2i, a1r, a1i)
        br, bi = cmul(a2r, a2i, b1r, b1i)
        return (ar, ai, br + b2r, bi + b2i)

    _, _, s_r, s_i = lax.associative_scan(combine, (a_r, a_i, bu_r, bu_i), axis=1)
    y = (jnp.einsum('ghp,blgp->blgh', c_re.astype(f32), s_r)
         - jnp.einsum('ghp,blgp->blgh', c_im.astype(f32), s_i)
         + d_skip.astype(f32) * ug)
    y = y.reshape(bsz, seq, SSM_WIDTH).astype(u.dtype)
    y = jax.nn.gelu(y)
    return y * jax.nn.sigmoid(y @ glu_w + glu_b)


def causal_dwconv(x, w):
    return lax.conv_general_dilated(
        x, w[:, None, :], window_strides=(1,), padding=[(CONV_K - 1, 0)],
        dimension_numbers=('NWC', 'WIO', 'NWC'), feature_group_count=x.shape[-1])


def l2norm(t):
    return t * lax.rsqrt(jnp.sum(t * t, axis=-1, keepdims=True) + L2_EPS)


def gated_deltanet_branch(q, k, v, z, beta_logit, a_in, conv_w, a_log, dt_bias, norm_w):
    f32 = jnp.float32
    bsz, seq, _ = q.shape
    n_chunks = seq // CHUNK
    qkv = jax.nn.silu(causal_dwconv(jnp.concatenate([q, k, v], axis=-1), conv_w)).astype(f32)
    q, k, v = jnp.split(qkv, 3, axis=-1)

    def heads(t):
        return t.reshape(bsz, n_chunks, CHUNK, GDN_HEADS, GDN_HEAD_DIM).transpose(0, 3, 1, 2, 4)

    def head_scalars(t):
        return t.reshape(bsz, n_chunks, CHUNK, GDN_HEADS).transpose(0, 3, 1, 2)

    q = l2norm(heads(q)) * (GDN_HEAD_DIM ** -0.5)
    k = l2norm(heads(k))
    v = heads(v)
    beta = head_scalars(jax.nn.sigmoid(beta_logit.astype(f32)))
    g = -jnp.exp(a_log.astype(f32)) * jax.nn.softplus(a_in.astype(f32) + dt_bias.astype(f32))
    gcum = jnp.cumsum(head_scalars(g), axis=-1)
    idx = jnp.arange(CHUNK)
    causal = idx[:, None] >= idx[None, :]
    strict = idx[:, None] > idx[None, :]
    decay = jnp.exp(jnp.where(causal, gcum[..., :, None] - gcum[..., None, :], -jnp.inf))
    k_beta = k * beta[..., None]
    lower = jnp.where(strict, jnp.einsum('bhncd,bhnsd->bhncs', k_beta, k) * decay, 0.0)
    rhs = jnp.concatenate([v * beta[..., None], k_beta * jnp.exp(gcum)[..., None]], axis=-1)
    sol = lax.linalg.triangular_solve(lower + jnp.eye(CHUNK, dtype=f32), rhs,
                                      left_side=True, lower=True, unit_diagonal=True)
    u_val, w_key = jnp.split(sol, 2, axis=-1)
    attn_intra = jnp.einsum('bhncd,bhnsd->bhncs', q, k) * decay
    q_dec = q * jnp.exp(gcum)[..., None]
    k_dec = k * jnp.exp(gcum[..., -1:] - gcum)[..., None]
    g_last = jnp.exp(gcum[..., -1])

    def chunk_step(state, xs):
        u_c, w_c, a_c, qd_c, kd_c, gl_c = xs
        v_new = u_c - jnp.einsum('bhcd,bhde->bhce', w_c, state)
        out = (jnp.einsum('bhcd,bhde->bhce', qd_c, state)
               + jnp.einsum('bhcs,bhse->bhce', a_c, v_new))
        state = state * gl_c[..., None, None] + jnp.einsum('bhcd,bhce->bhde', kd_c, v_new)
        return state, out

    xs = tuple(jnp.moveaxis(t, 2, 0) for t in (u_val, w_key, attn_intra, q_dec, k_dec, g_last))
    state0 = jnp.zeros((bsz, GDN_HEADS, GDN_HEAD_DIM, GDN_HEAD_DIM), f32)
    _, o = lax.scan(chunk_step, state0, xs)
    o = o.transpose(1, 0, 3, 2, 4).reshape(bsz, seq, GDN_HEADS, GDN_HEAD_DIM)
    o = o * lax.rsqrt(jnp.mean(o * o, axis=-1, keepdims=True) + RMS_EPS) * norm_w.astype(f32)
    o = o * jax.nn.silu(z.astype(f32).reshape(bsz, seq, GDN_HEADS, GDN_HEAD_DIM))
    return o.reshape(bsz, seq, GDN_WIDTH).astype(q.dtype if q.dtype != f32 else z.dtype)


def hybrid_mixer(h, w_in, conv_w, ssm_a_re, ssm_a_im, ssm_log_dt, ssm_b_re, ssm_b_im, ssm_c_re,
                 ssm_c_im, ssm_d, glu_w, glu_b, gdn_a_log, gdn_dt_bias, gdn_norm_w,
                 w_br_ssm, w_br_gdn, w_out):
    offsets = np.cumsum(IN_SIZES)[:-1].tolist()
    u, q, k, v, z, beta_logit, a_in, gate_ssm, gate_gdn = jnp.split(h @ w_in, offsets, axis=-1)
    y_ssm = s5_branch(u, ssm_a_re, ssm_a_im, ssm_log_dt, ssm_b_re, ssm_b_im,
                      ssm_c_re, ssm_c_im, ssm_d, glu_w, glu_b)
    y_gdn = gated_deltanet_branch(q, k, v, z, beta_logit, a_in, conv_w,
                                  gdn_a_log, gdn_dt_bias, gdn_norm_w)
    merged = (jax.nn.sigmoid(gate_ssm) * (y_ssm @ w_br_ssm)
              + jax.nn.sigmoid(gate_gdn) * (y_gdn @ w_br_gdn))
    return merged @ w_out


def setup_inputs(seed: int = 0) -> dict:
    key = jax.random.key(seed)
    ks = jax.random.split(key, 32)
    f32 = jnp.float32
    L = DEPTH
    dn_beta = (8.0 * DEPTH) ** -0.25

    def nrm(k, shape, scale):
        return scale * jax.random.normal(k, shape, f32)

    lo, hi = float(np.log(DT_MIN)), float(np.log(DT_MAX))
    gdn_dt = jnp.exp(jax.random.uniform(ks[20], (L, GDN_HEADS), f32, lo, hi))
    return {
        'x': nrm(ks[0], (BATCH, SEQ, D_MODEL), 1.0),
        'ffn1_w_gu': nrm(ks[1], (L, D_MODEL, 2 * D_FF), D_MODEL ** -0.5),
        'ffn1_w_down': nrm(ks[2], (L, D_FF, D_MODEL), dn_beta * D_FF ** -0.5),
        'ln1_g': 1.0 + nrm(ks[3], (L, D_MODEL), 0.02),
        'ln1_b': nrm(ks[4], (L, D_MODEL), 0.02),
        'w_in': nrm(ks[5], (L, D_MODEL, IN_COLS), D_MODEL ** -0.5),
        'conv_w': nrm(ks[6], (L, CONV_K, 3 * GDN_WIDTH), CONV_K ** -0.5),
        'ssm_a_re': -0.5 + nrm(ks[7], (L, SSM_GROUPS, SSM_STATE), 0.02),
        'ssm_a_im': jnp.pi * jnp.arange(SSM_STATE, dtype=f32) + nrm(ks[8], (L, SSM_GROUPS, SSM_STATE), 0.02),
        'ssm_log_dt': jax.random.uniform(ks[9], (L, SSM_GROUPS), f32, lo, hi),
        'ssm_b_re': nrm(ks[10], (L, SSM_GROUPS, SSM_STATE, SSM_GROUP), (2 * SSM_GROUP) ** -0.5),
        'ssm_b_im': nrm(ks[11], (L, SSM_GROUPS, SSM_STATE, SSM_GROUP), (2 * SSM_GROUP) ** -0.5),
        'ssm_c_re': nrm(ks[12], (L, SSM_GROUPS, SSM_GROUP, SSM_STATE), (2 * SSM_STATE) ** -0.5),
        'ssm_c_im': nrm(ks[13], (L, SSM_GROUPS, SSM_GROUP, SSM_STATE), (2 * SSM_STATE) ** -0.5),
        'ssm_d': nrm(ks[14], (L, SSM_GROUPS, SSM_GROUP), 1.0),
        'glu_w': nrm(ks[15], (L, SSM_WIDTH, SSM_WIDTH), SSM_WIDTH ** -0.5),
        'glu_b': nrm(ks[16], (L, SSM_WIDTH), 0.02),
        'gdn_a_log': jnp.log(jax.random.uniform(ks[17], (L, GDN_HEADS), f32, 1.0, 16.0)),
        'gdn_dt_bias': gdn_dt + jnp.log(-jnp.expm1(-gdn_dt)),
        'gdn_norm_w': 1.0 + nrm(ks[18], (L, GDN_HEAD_DIM), 0.02),
        'w_br_ssm': nrm(ks[19], (L, SSM_WIDTH, D_MODEL), SSM_WIDTH ** -0.5),
        'w_br_gdn': nrm(ks[21], (L, GDN_WIDTH, D_MODEL), GDN_WIDTH ** -0.5),
        'w_out': nrm(ks[22], (L, D_MODEL, D_MODEL), dn_beta * D_MODEL ** -0.5),
        'ln2_g': 1.0 + nrm(ks[23], (L, D_MODEL), 0.02),
        'ln2_b': nrm(ks[24], (L, D_MODEL), 0.02),
        'ffn2_w_gu': nrm(ks[25], (L, D_MODEL, 2 * D_FF), D_MODEL ** -0.5),
        'ffn2_w_down': nrm(ks[26], (L, D_FF, D_MODEL), dn_beta * D_FF ** -0.5),
        'ln3_g': 1.0 + nrm(ks[27], (L, D_MODEL), 0.02),
        'ln3_b': nrm(ks[28], (L, D_MODEL), 0.02),
    }


def reference(x, ffn1_w_gu, ffn1_w_down, ln1_g, ln1_b, w_in, conv_w, ssm_a_re, ssm_a_im,
              ssm_log_dt, ssm_b_re, ssm_b_im, ssm_c_re, ssm_c_im, ssm_d, glu_w, glu_b,
              gdn_a_log, gdn_dt_bias, gdn_norm_w, w_br_ssm, w_br_gdn, w_out, ln2_g, ln2_b,
              ffn2_w_gu, ffn2_w_down, ln3_g, ln3_b):
    alpha = (2.0 * DEPTH) ** 0.25
    for l in range(DEPTH):
        x = layer_norm(alpha * x + 0.5 * swiglu_ffn(x, ffn1_w_gu[l], ffn1_w_down[l]), ln1_g[l], ln1_b[l])
        mix = hybrid_mixer(x, w_in[l], conv_w[l], ssm_a_re[l], ssm_a_im[l], ssm_log_dt[l],
                           ssm_b_re[l], ssm_b_im[l], ssm_c_re[l], ssm_c_im[l], ssm_d[l],
                           glu_w[l], glu_b[l], gdn_a_log[l], gdn_dt_bias[l], gdn_norm_w[l],
                           w_br_ssm[l], w_br_gdn[l], w_out[l])
        x = layer_norm(alpha * x + mix, ln2_g[l], ln2_b[l])
        x = layer_norm(alpha * x + 0.5 * swiglu_ffn(x, ffn2_w_gu[l], ffn2_w_down[l]), ln3_g[l], ln3_b[l])
    return x
```

```python
import numpy as np
from contextlib import ExitStack
import concourse.bass as bass
import concourse.mybir as mybir
from concourse.bass_utils import run_bass_kernel_spmd

F32 = mybir.dt.float32
BF16 = mybir.dt.bfloat16
AF = mybir.ActivationFunctionType
ALU = mybir.AluOpType

D_MODEL = 2048
DEPTH = 4
D_FF = 5632
SEQ = 2048
BATCH = 4
NCORES = 8
T = 1024
KC = D_MODEL // 128
ALPHA = (2.0 * DEPTH) ** 0.25
LN_EPS = 1e-5

ENGS = ("pe", "act", "dve", "pool", "sp")


class Prog:
    def __init__(self, nc, same_engine_sync=True):
        self.nc = nc
        self.same_engine_sync = same_engine_sync
        self.streams = {e: [] for e in ENGS}
        self.last_write = {}
        self.readers = {}
        self.marked = {e: set() for e in ENGS}
        self.dma_sems = {}
        self.stack = None
        self.pending = {}
        self.alias = {}
        self.shared_sems = set()

    def barrier(self, markers):
        evs = []
        for e, (fn, reads, writes) in markers.items():
            evs.append(self.op(e, fn, reads=reads, writes=writes))
        for name, ent in self.dma_sems.items():
            if ent[1] > 0:
                evs.append(("dma", name, ent[1]))
        for e in ENGS:
            self.pending[e] = list(evs)

    def op(self, eng, fn, reads=(), writes=(), dma_sem=None):
        idx = len(self.streams[eng])
        deps = []
        for k in reads:
            for kk in (k,) + tuple(self.alias.get(k, ())):
                ev = self.last_write.get(kk)
                if ev is not None:
                    deps.append(ev)
        for k in writes:
            for kk in (k,) + tuple(self.alias.get(k, ())):
                ev = self.last_write.get(kk)
                if ev is not None:
                    deps.append(ev)
                deps.extend(self.readers.get(kk, ()))
        if self.pending.get(eng):
            deps.extend(self.pending[eng])
            self.pending[eng] = []
        waits = {}
        for ev in deps:
            if ev[0] == "eng":
                if ev[1] == eng and (eng in ("pe", "sp") or not self.same_engine_sync):
                    continue
                if ev[1] == eng and ev[2] >= idx:
                    continue
                key = ("eng", ev[1])
                if waits.get(key, -1) < ev[2]:
                    waits[key] = ev[2]
            else:
                key = ("dma", ev[1])
                v = ev[2]
                if ev[1] in self.shared_sems:
                    v = self.dma_sems[ev[1]][1]
                if waits.get(key, -1) < v:
                    waits[key] = v
        for key, v in waits.items():
            if key[0] == "eng":
                self.marked[key[1]].add(v)
        if dma_sem is not None:
            ent = self.dma_sems[dma_sem]
            ent[1] += ent[2]
            myev = ("dma", dma_sem, ent[1])
        else:
            myev = ("eng", eng, idx)
        self.streams[eng].append(dict(waits=waits, fn=fn, dma=dma_sem))
        for k in reads:
            self.readers.setdefault(k, []).append(myev)
        for k in writes:
            self.last_write[k] = myev
            self.readers[k] = []
        return myev

    def seal_dma_group(self, name):
        tot = self.dma_sems[name][1]
        for k, ev in list(self.last_write.items()):
            if ev[0] == "dma" and ev[1] == name:
                self.last_write[k] = ("dma", name, tot)

    def new_dma_sem(self, name, inc=16):
        sem = self.stack.enter_context(self.nc.semaphore("d_" + name))
        self.dma_sems[name] = [sem, 0, inc]

    def emit(self, final_events):
        nc = self.nc
        esem = {e: self.stack.enter_context(nc.semaphore("e_" + e)) for e in ENGS}
        rank = {}
        for e in ENGS:
            r = 0
            rk = {}
            for i in range(len(self.streams[e])):
                if i in self.marked[e]:
                    r += 1
                    rk[i] = r
            rank[e] = rk
        handles = {"pe": "tensor", "act": "scalar", "dve": "vector", "pool": "gpsimd", "sp": "sync"}
        streams = self.streams
        marked = self.marked
        dma_sems = self.dma_sems

        def run(ename, eng):
            waited = {}
            for i, it in enumerate(streams[ename]):
                for key, v in it["waits"].items():
                    if key[0] == "eng":
                        sem = esem[key[1]]
                        val = rank[key[1]][v]
                    else:
                        sem = dma_sems[key[1]][0]
                        val = v
                    if waited.get(key, -1) >= val:
                        continue
                    waited[key] = val
                    eng.wait_ge(sem, val)
                ins = it["fn"](eng)
                if it["dma"] is not None:
                    ins.then_inc(dma_sems[it["dma"]][0], dma_sems[it["dma"]][2])
                elif i in marked[ename]:
                    ins.then_inc(esem[ename], 1)
            if ename == "sp":
                for ev in final_events:
                    if ev[0] == "dma":
                        eng.wait_ge(dma_sems[ev[1]][0], ev[2])
                    else:
                        eng.wait_ge(esem[ev[1]], rank[ev[1]][ev[2]])

        for ev in final_events:
            if ev[0] == "eng":
                assert ev[2] in marked[ev[1]]
        with nc.Block() as block:
            @block.tensor
            def _(eng):
                run("pe", eng)

            @block.scalar
            def _(eng):
                run("act", eng)

            @block.vector
            def _(eng):
                run("dve", eng)

            @block.gpsimd
            def _(eng):
                run("pool", eng)

            @block.sync
            def _(eng):
                run("sp", eng)


IN_U0, IN_Q0, IN_K0, IN_V0, IN_Z0, IN_B0, IN_A0, IN_GS0, IN_GG0 = 0, 1024, 2048, 3072, 4096, 5120, 5128, 5136, 7184
NGU = D_FF // 128
NST = 11
FCS = NGU // NST
NDB = D_MODEL // 256
RING = KC * 256
SW = 1024


def small_layout(L):
    off = {}
    n = 0

    def add(name, size):
        nonlocal n
        off[name] = n
        n += size
    add("flag", 1)
    add("maskE", 1)
    add("maskO", 1)
    add("negpi", 1)
    for l in range(L):
        add(("ln", l), 3 * 2 * KC)
        add(("convw", l), 24 * 4)
        add(("glub", l), 8)
        add(("ssmd", l), 8)
        add(("normw", l), 1)
        add(("alog", l), 1)
        add(("dtb", l), 1)
        add(("s5b", l), 96)
    off["_n"] = n
    return off


def build_program(n_layers=DEPTH, ncores=NCORES, do_mixer=True, do_gdn=True, dumps=()):
    nc = bass.Bass("TRN2", target_bir_lowering=False)
    L = n_layers
    SO = small_layout(L)
    dram_in = lambda name, shape: nc.dram_tensor(name, shape, F32, kind="ExternalInput").ap()
    xin = dram_in("xT", [D_MODEL, T])
    yout = nc.dram_tensor("yT", [D_MODEL, T], F32, kind="ExternalOutput").ap()
    wgu = [dram_in(f"wgu{f}", [L, NGU, 128, RING]) for f in range(2)]
    wdn = [dram_in(f"wdn{f}", [L, NST, NDB, 128, FCS * 256]) for f in range(2)]
    small_d = dram_in("small", [128, SO["_n"]])
    consts_d = dram_in("consts", [128, 320])
    w_u = dram_in("w_u", [L, 4, 128, RING])
    w_qkvz = dram_in("w_qkvz", [L, 16, 128, RING])
    w_ba = dram_in("w_ba", [L, 128, KC * 64])
    w_gate = dram_in("w_gate", [L, 16, 128, RING])
    w_br = dram_in("w_br", [L, 16, 128, 2048])
    w_o = dram_in("w_o", [L, 8, 128, RING])
    w_glu = dram_in("w_glu", [L, 4, 128, 2048])
    s5a_d = dram_in("s5a", [L, 128, 5 * 512])
    s5c_d = dram_in("s5c", [L, 128, 2048])
    xsp_d = nc.dram_tensor("xsp", [128, KC * T], F32, kind="Internal").ap()
    halo_src = nc.dram_tensor("halo_src", [128, KC * 3], F32, kind="Internal").ap()
    halo_all = nc.dram_tensor("halo_all", [256, KC * 3], F32, kind="Internal").ap()
    NSTATE = 64 + 8 * 128
    st_src = nc.dram_tensor("st_src", [128, NSTATE], F32, kind="Internal").ap()
    st_all = nc.dram_tensor("st_all", [256, NSTATE], F32, kind="Internal").ap()
    groups = [[2 * i, 2 * i + 1] for i in range(ncores // 2)]
    dump_d = {}

    stack = ExitStack()
    with stack:
        pg = Prog(nc)
        pg.stack = stack
        sb = lambda name, shape, dt: stack.enter_context(nc.sbuf_tensor(name, shape, dt))
        x32 = sb("x32", [128, KC, T], F32)
        xbf = sb("xbf", [128, KC, T], BF16)
        xhbf = sb("xhbf", [128, KC, 4], BF16)
        small = sb("small_sb", [128, SO["_n"]], F32)
        ones = sb("ones", [128, 128], F32)
        onesb = sb("onesb", [128, 128], BF16)
        epsc = sb("epsc", [128, 1], F32)
        bar = sb("bar", [128, 8], F32)
        onec = sb("onec", [128, 1], F32)
        eps6 = sb("eps6", [128, 1], F32)
        consts = sb("consts_sb", [128, 320], F32)
        ident = consts[:, 0:128]
        maskSL, maskSU, maskIU = consts[0:64, 128:192], consts[0:64, 192:256], consts[0:64, 256:320]
        glt = sb("glt", [128, 128], F32)
        gst = sb("gst", [128, 512], F32)
        gstb = sb("gstb", [128, 512], BF16)
        gvn = sb("gvn", [128, 512], BF16)
        for b_ in range(3):
            for i_ in range(4):
                pg.alias[("psg", b_, i_)] = [("ps", b_)]
            pg.alias[("ps", b_)] = [("psg", b_, i_) for i_ in range(4)]
        AW = 24576
        arena = sb("arena", [128, AW], F32)
        psall = stack.enter_context(nc.psum_tensor("psall", [128, 8 * 512], F32))
        ps = [psall[:, i * 512:(i + 1) * 512] for i in range(8)]
        pg.new_dma_sem("setup")
        pg.new_dma_sem("out")
        pg.new_dma_sem("misc")
        pg.shared_sems.add("misc")

        def awords(a, n):
            return arena[:, a:a + n]
        abuf = [awords(i * 2048, 2048).bitcast(BF16).rearrange("p (j t) -> p j t", j=FCS) for i in range(2)]
        wdn_base = 4096
        sq = awords(5632, 512)
        stat = awords(6144, 2048).rearrange("p (a t) -> p a t", a=4)
        sg = [awords(8192 + i * 512, 512) for i in range(2)]
        tmpn = [awords(9216 + i * 512, 512) for i in range(2)]

        class Ring:
            def __init__(self, name, base, nslots, words):
                self.name, self.base, self.n, self.words = name, base, nslots, words
                self.epoch = -1
                self.new_epoch()
                self.i = 0

            def new_epoch(self):
                self.epoch += 1
                for s in range(self.n):
                    pg.new_dma_sem(f"{self.name}{self.epoch}_{s}")

            def load(self, src_ap, elems=None):
                s = self.i % self.n
                self.i += 1
                elems = self.words * 2 if elems is None else elems
                dst = awords(self.base + s * self.words, self.words).bitcast(BF16)[:, 0:elems]
                pg.op("pool", lambda e, dst=dst, src=src_ap: e.dma_start(out=dst, in_=src),
                      writes=[(self.name, s)], dma_sem=f"{self.name}{self.epoch}_{s}")
                return s

            def view(self, s):
                return awords(self.base + s * self.words, self.words).bitcast(BF16)

            def key(self, s):
                return (self.name, s)

        wr_gu = Ring("wgu", 16384, 3, 2048)
        wr_dn = Ring("wdn", wdn_base, 3, 512)
        wr_br = Ring("wbr", 22528, 2, 1024)

        x32flat = x32[:].rearrange("p k t -> p (k t)")

        def slot(i, words=SW, off=0):
            if i < 16:
                return x32flat[:, i * SW + off: i * SW + off + words]
            return awords((i - 16) * SW + off, words)

        def sk(*idx):
            return [("sl", i) for i in idx]

        def sm(name, n=1, o=0):
            return small[:, SO[name] + o: SO[name] + o + n]

        def mm(out, lhsT, rhs, start, stop, r, w, **kw):
            return pg.op("pe", lambda e: e.matmul(out, lhsT, rhs, start=start, stop=stop, **kw), reads=r, writes=w)

        def act(out, in_, func, r, w, bias=None, scale=1.0):
            if bias is None:
                return pg.op("act", lambda e: e.activation(out=out, in_=in_, func=func, scale=scale), reads=r, writes=w)
            return pg.op("act", lambda e: e.activation(out=out, in_=in_, func=func, bias=bias, scale=scale),
                         reads=r, writes=w)

        def tt(eng, out, in0, in1, op, r, w):
            return pg.op(eng, lambda e: e.tensor_tensor(out=out, in0=in0, in1=in1, op=op), reads=r, writes=w)

        def ts(eng, out, in0, s1, s2, op0, op1, r, w):
            if s2 is None:
                return pg.op(eng, lambda e: e.tensor_scalar(out=out, in0=in0, scalar1=s1, scalar2=None, op0=op0),
                             reads=r, writes=w)
            return pg.op(eng, lambda e: e.tensor_scalar(out=out, in0=in0, scalar1=s1, scalar2=s2, op0=op0, op1=op1),
                         reads=r, writes=w)

        def stt(eng, out, in0, scalar, in1, op0, op1, r, w):
            return pg.op(eng, lambda e: e.scalar_tensor_tensor(out=out, in0=in0, scalar=scalar, in1=in1, op0=op0, op1=op1),
                         reads=r, writes=w)

        def cp(eng, out, in_, r, w):
            if eng == "act":
                return pg.op("act", lambda e: e.copy(out=out, in_=in_), reads=r, writes=w)
            return pg.op(eng, lambda e: e.tensor_copy(out=out, in_=in_), reads=r, writes=w)

        misc_name = ["misc"]

        def dma(q, out, in_, r, w, sem="misc"):
            if sem == "misc":
                sem = misc_name[0]
            return pg.op(q, lambda e: e.dma_start(out=out, in_=in_), reads=r, writes=w, dma_sem=sem)

        def dump(name, ap, shape, r):
            if name not in dumps:
                return
            d = nc.dram_tensor("dump_" + name, list(shape), ap.dtype, kind="ExternalOutput").ap()
            dump_d[name] = d
            dma("sp", d, ap, r, [], sem="out")

        def barrier():
            pg.barrier({
                "pe": (lambda e: e.matmul(ps[7][0:32, 0:2], onesb[0:32, 0:32], onesb[0:32, 0:2], start=True, stop=True),
                       ["onesb"], [("ps", 7)]),
                "act": (lambda e: e.copy(out=bar[:, 0:1], in_=epsc[:, 0:1]), ["epsc"], ["bar0"]),
                "dve": (lambda e: e.memset(bar[:, 1:2], 0.0), [], ["bar1"]),
                "pool": (lambda e: e.memset(bar[:, 2:3], 0.0), [], ["bar2"]),
            })

        XK = [("x32", kc, th) for kc in range(KC) for th in range(2)]
        XBK = [("xbf", kc, th) for kc in range(KC) for th in range(2)]

        xin_v = xin.rearrange("(kc p) t -> p kc t", p=128)
        for kc in range(KC):
            dma("sp", x32[:, kc, :], xin_v[:, kc, :], [], [("x32", kc, 0), ("x32", kc, 1)], sem="setup")
        dma("sp", small[:], small_d, [], ["small"], sem="setup")
        dma("sp", consts[:], consts_d, [], ["consts"], sem="setup")
        pg.seal_dma_group("setup")
        pg.op("dve", lambda e: e.memset(ones[:], 1.0), writes=["ones"])
        pg.op("dve", lambda e: e.memset(onesb[:], 1.0), writes=["onesb"])
        pg.op("dve", lambda e: e.memset(epsc[:], float(LN_EPS)), writes=["epsc"])
        pg.op("dve", lambda e: e.memset(onec[:], 1.0), writes=["onec"])
        pg.op("dve", lambda e: e.memset(eps6[:], 1e-6), writes=["eps6"])
        for kc in range(KC):
            for th in range(2):
                sl = slice(th * 512, (th + 1) * 512)
                cp("act", xbf[:, kc, sl], x32[:, kc, sl], [("x32", kc, th)], [("xbf", kc, th)])

        psrr = [0]

        def ffn(l, f):
            for kc in range(KC):
                for th in range(2):
                    sl = slice(th * 512, (th + 1) * 512)
                    ts("pool", x32[:, kc, sl], x32[:, kc, sl], float(ALPHA), None, ALU.mult, None,
                       [("x32", kc, th)], [("x32", kc, th)])
            gu_loaded = {}
            dn_loaded = {}

            def load_gu(j):
                gu_loaded[j] = wr_gu.load(wgu[f][l, j])

            def load_dn(s, db):
                dn_loaded[(s, db)] = wr_dn.load(wdn[f][l, s, db])

            def gu_chunk(j, ab, jj):
                slot_ = gu_loaded.pop(j)
                wv = wr_gu.view(slot_)
                for th in range(2):
                    sl = slice(th * 512, (th + 1) * 512)
                    pgate = psrr[0] % 4
                    pup = (psrr[0] + 1) % 4
                    psrr[0] += 2
                    for kc in range(KC):
                        mm(ps[pgate], wv[:, kc * 256: kc * 256 + 128], xbf[:, kc, sl], kc == 0, kc == KC - 1,
                           [wr_gu.key(slot_), ("xbf", kc, th)], [("ps", pgate)])
                    for kc in range(KC):
                        mm(ps[pup], wv[:, kc * 256 + 128: kc * 256 + 256], xbf[:, kc, sl], kc == 0, kc == KC - 1,
                           [wr_gu.key(slot_), ("xbf", kc, th)], [("ps", pup)])
                    sgi = (psrr[0] // 2) % 2
                    act(sg[sgi], ps[pgate], AF.Silu, [("ps", pgate)], [("sg", sgi)])
                    tt("dve", abuf[ab][:, jj, sl], sg[sgi], ps[pup], ALU.mult,
                       [("sg", sgi), ("ps", pup)], [("abuf", ab, jj, th)])

            def down_stage(s, ab):
                for db in range(NDB):
                    if db + 1 < NDB:
                        load_dn(s, db + 1)
                    slot_ = dn_loaded.pop((s, db))
                    wv = wr_dn.view(slot_)
                    for dd in range(2):
                        dc = db * 2 + dd
                        for th in range(2):
                            sl = slice(th * 512, (th + 1) * 512)
                            pb = 4 + (psrr[0] % 2)
                            psrr[0] += 1
                            for jj in range(FCS):
                                mm(ps[pb], wv[:, jj * 256 + dd * 128: jj * 256 + dd * 128 + 128], abuf[ab][:, jj, sl],
                                   jj == 0, jj == FCS - 1, [wr_dn.key(slot_), ("abuf", ab, jj, th)], [("ps", pb)])
                            stt("dve", x32[:, dc, sl], ps[pb], 0.5, x32[:, dc, sl], ALU.mult, ALU.add,
                                [("ps", pb), ("x32", dc, th)], [("x32", dc, th)])

            load_gu(0)
            load_gu(1)
            for s in range(NST):
                ab = s % 2
                load_dn(s, 0)
                for jj in range(FCS):
                    j = s * FCS + jj
                    if j + 2 < NGU:
                        load_gu(j + 2)
                    gu_chunk(j, ab, jj)
                down_stage(s, ab)

        def layer_norm(l, i):
            base = SO[("ln", l)] + i * 2 * KC
            for th in range(2):
                sl = slice(th * 512, (th + 1) * 512)
                for kc in range(KC):
                    mm(ps[6], ones[:], x32[:, kc, sl], kc == 0, kc == KC - 1, ["ones", ("x32", kc, th)], [("ps", 6)])
                for kc in range(KC):
                    act(sq, x32[:, kc, sl], AF.Square, [("x32", kc, th)], ["sq"])
                    mm(ps[7], ones[:], sq, kc == 0, kc == KC - 1, ["ones", "sq"], [("ps", 7)])
                mean = stat[:, 0, :]
                rstd = stat[:, 1, :]
                tmp = stat[:, 2, :]
                ts("dve", mean, ps[6], 1.0 / D_MODEL, None, ALU.mult, None, [("ps", 6)], ["mean"])
                tt("dve", tmp, mean, mean, ALU.mult, ["mean"], ["tmp"])
                stt("dve", rstd, ps[7], 1.0 / D_MODEL, tmp, ALU.mult, ALU.subtract, [("ps", 7), "tmp"], ["rstd"])
                act(rstd, rstd, AF.Sqrt, ["rstd", "epsc"], ["rstd"], bias=epsc[:, 0:1])
                pg.op("dve", lambda e: e.reciprocal(out=rstd, in_=rstd), reads=["rstd"], writes=["rstd"])
                for kc in range(KC):
                    ti = kc % 2
                    tt("dve", tmpn[ti], x32[:, kc, sl], mean, ALU.subtract, [("x32", kc, th), "mean"], [("tmpn", ti)])
                    tt("dve", tmpn[ti], tmpn[ti], rstd, ALU.mult, [("tmpn", ti), "rstd"], [("tmpn", ti)])
                    act(x32[:, kc, sl], tmpn[ti], AF.Identity, [("tmpn", ti), "small"], [("x32", kc, th)],
                        bias=small[:, base + KC + kc: base + KC + kc + 1], scale=small[:, base + kc: base + kc + 1])
                    cp("pool", xbf[:, kc, sl], x32[:, kc, sl], [("x32", kc, th)], [("xbf", kc, th)])

        def cis(eng, c, s, th, tA, tB, keys_c, keys_s, keys_th, keys_tA, keys_tB):
            act(s, th, AF.Sin, keys_th, keys_s, scale=1.0 / 16)
            act(tA, th, AF.Sin, keys_th, keys_tA, scale=1.0 / 32)
            tt(eng, tA, tA, tA, ALU.mult, keys_tA, keys_tA)
            ts(eng, c, tA, -2.0, 1.0, ALU.mult, ALU.add, keys_tA, keys_c)
            for _ in range(4):
                tt(eng, tA, c, c, ALU.mult, keys_c, keys_tA)
                tt(eng, tB, s, s, ALU.mult, keys_s, keys_tB)
                tt(eng, s, c, s, ALU.mult, keys_c + keys_s, keys_s)
                ts(eng, s, s, 2.0, None, ALU.mult, None, keys_s, keys_s)
                tt(eng, c, tA, tB, ALU.subtract, keys_tA + keys_tB, keys_c)

        def cmul(eng, or_, oi, ar, ai, br, bi, t1, t2, k_or, k_oi, k_a, k_b, k_t1, k_t2):
            tt(eng, t1, ar, br, ALU.mult, k_a + k_b, k_t1)
            tt(eng, t2, ai, bi, ALU.mult, k_a + k_b, k_t2)
            tt(eng, or_, t1, t2, ALU.subtract, k_t1 + k_t2, k_or)
            tt(eng, t1, ar, bi, ALU.mult, k_a + k_b, k_t1)
            tt(eng, t2, ai, br, ALU.mult, k_a + k_b, k_t2)
            tt(eng, oi, t1, t2, ALU.add, k_t1 + k_t2, k_oi)

        def T2(i, h):
            return slot(i, 512, h * 512)

        def v3(ap, inner):
            return ap.rearrange("p (a b) -> p a b", b=inner)

        def bc_last(ap2, n):
            return ap2.unsqueeze(2).to_broadcast([ap2.shape[0], ap2.shape[1], n])

        def bc_mid(ap2, n):
            return ap2.unsqueeze(1).to_broadcast([ap2.shape[0], n, ap2.shape[1]])

        u_bf = awords(0, 4096).bitcast(BF16).rearrange("p (c t) -> p c t", c=8)
        y1 = awords(4096, 4096).bitcast(BF16).rearrange("p (c t) -> p c t", c=8)
        y2 = awords(8192, 4096).bitcast(BF16).rearrange("p (c t) -> p c t", c=8)
        ygdn = awords(12288, 4096).bitcast(BF16).rearrange("p (c t) -> p c t", c=8)
        merged = awords(0, 8192).bitcast(BF16).rearrange("p (c t) -> p c t", c=16)

        def kU(ct):
            return ("sl", 16 + ct // 2)

        def kY1(ct):
            return ("sl", 20 + ct // 2)

        def kY2(ct):
            return ("sl", 24 + ct // 2)

        def kYG(ct):
            return ("sl", 28 + ct // 2)

        def kM(dc):
            return ("sl", 16 + dc // 2)

        def bsm(i):
            return slot(31, 32, 512 + 32 * i)
        KB = sk(31)

        cc_sems = []

        def collective(src, dst, r, w):
            name = f"cc{len(cc_sems)}"
            pg.new_dma_sem(name, inc=1)
            cc_sems.append(name)
            return pg.op("pool", lambda e: e.collective_compute("AllGather", ALU.bypass, replica_groups=groups,
                                                                ins=[src], outs=[dst]),
                         reads=r, writes=w, dma_sem=name)

        def proj_block_load(dram_block):
            return wr_gu.load(dram_block)

        def halo_and_spill(l):
            x3 = x32[:, :, T - 3:T]
            dma("sp", halo_src.rearrange("p (k c) -> p k c", c=3), x3, [("x32", kc, 1) for kc in range(KC)], ["halo_src"])
            collective(halo_src, halo_all, ["halo_src"], ["halo_all"])
            hal = slot(15, 64)
            dma("sp", xsp_d, x32flat, XK, ["xsp"])
            barrier()
            dma("sp", hal[:, 0:KC * 3], halo_all[0:128, :], ["halo_all"], sk(15))
            ts("dve", hal[:, 0:KC * 3], hal[:, 0:KC * 3], sm("flag"), None, ALU.mult, None, sk(15) + ["small"], sk(15))
            pg.op("dve", lambda e: e.memset(xhbf[:], 0.0), writes=["xhbf"])
            cp("dve", xhbf[:, :, 0:3], hal[:, 0:KC * 3].rearrange("p (k c) -> p k c", c=3), sk(15) + ["xhbf"], ["xhbf"])

        def s5_prep(l, full):
            E = ALU
            if full:
                dma("sp", x32flat[:, 10 * SW: 10 * SW + 2560], s5a_d[l], [], sk(10, 11, 12))
                lrA, liA, ldtA, breA, bimA = T2(10, 0), T2(10, 1), T2(11, 0), T2(11, 1), T2(12, 0)
                KA = sk(10, 11, 12)
                dtA, thA, magA, cosA, sinA = T2(12, 1), T2(13, 0), T2(13, 1), T2(14, 0), T2(14, 1)
                tA, tB, lbr, lbi, den, zr, zi, bbr = T2(4, 0), T2(4, 1), T2(5, 0), T2(5, 1), T2(6, 0), T2(6, 1), T2(7, 0), T2(7, 1)
                bbi = T2(13, 0)
                act(dtA, ldtA, AF.Exp, KA, sk(12))
                tt("dve", thA, liA, dtA, E.mult, KA + sk(12), sk(13))
                tt("dve", magA, lrA, dtA, E.mult, KA + sk(12), sk(13))
                act(magA, magA, AF.Exp, sk(13), sk(13))
                cis("dve", cosA, sinA, thA, tA, tB, sk(14), sk(14), sk(13), sk(4), sk(4))
                tt("dve", lbr, magA, cosA, E.mult, sk(13, 14), sk(5))
                ts("dve", lbr, lbr, -1.0, None, E.add, None, sk(5), sk(5))
                tt("dve", lbi, magA, sinA, E.mult, sk(13, 14), sk(5))
                tt("dve", den, lrA, lrA, E.mult, KA, sk(6))
                tt("dve", tB, liA, liA, E.mult, KA, sk(4))
                tt("dve", den, den, tB, E.add, sk(6, 4), sk(6))
                pg.op("dve", lambda e: e.reciprocal(out=den, in_=den), reads=sk(6), writes=sk(6))
                tt("dve", tA, lbr, lrA, E.mult, sk(5) + KA, sk(4))
                tt("dve", tB, lbi, liA, E.mult, sk(5) + KA, sk(4))
                tt("dve", zr, tA, tB, E.add, sk(4), sk(6))
                tt("dve", zr, zr, den, E.mult, sk(6), sk(6))
                tt("dve", tA, lbi, lrA, E.mult, sk(5) + KA, sk(4))
                tt("dve", tB, lbr, liA, E.mult, sk(5) + KA, sk(4))
                tt("dve", zi, tA, tB, E.subtract, sk(4), sk(7))
                tt("dve", zi, zi, den, E.mult, sk(7, 6), sk(7))
                cmul("dve", bbr, bbi, zr, zi, breA, bimA, tA, tB, sk(7), sk(13), sk(6, 7), KA, sk(4), sk(4))
                BBr = v3(slot(29, 512, 0).bitcast(BF16), 128)
                BBi = v3(slot(29, 512, 512).bitcast(BF16), 128)
                for dst, src, ksrc in ((BBr, bbr, sk(7)), (BBi, bbi, sk(13))):
                    s3 = v3(src, 64)
                    ts("dve", dst[:, :, 0:64], s3, sm("maskE"), None, E.mult, None, ksrc + ["small"], sk(29))
                    ts("dve", dst[:, :, 64:128], s3, sm("maskO"), None, E.mult, None, ksrc + ["small"], sk(29))
                cst = x32flat[:, 4 * SW: 6 * SW]
                dma("sp", cst, s5c_d[l], [], sk(4, 5))
                CCr = slot(30, 512, 0).bitcast(BF16)
                nCCr = slot(30, 512, 512).bitcast(BF16)
                nCCi = slot(31, 512, 0).bitcast(BF16)
                cp("dve", CCr, cst[:, 0:1024], sk(4), sk(30))
                ts("dve", nCCr, cst[:, 0:1024], -1.0, None, E.mult, None, sk(4), sk(30))
                ts("dve", nCCi, cst[:, 1024:2048], -1.0, None, E.mult, None, sk(5), sk(31))
            lrB, liB, ldtB = sm(("s5b", l), 32, 0), sm(("s5b", l), 32, 32), sm(("s5b", l), 32, 64)
            dtB, thB, rB, cB, sB = bsm(0), bsm(1), bsm(2), bsm(3), bsm(4)
            act(dtB, ldtB, AF.Exp, ["small"], KB)
            tt("dve", thB, liB, dtB, E.mult, ["small"] + KB, KB)
            tt("dve", rB, lrB, dtB, E.mult, ["small"] + KB, KB)
            act(rB, rB, AF.Exp, KB, KB)
            cis("dve", cB, sB, thB, bsm(5), bsm(6), KB, KB, KB, KB, KB)
            Ec, Es, Rp, Mr = (v3(slot(i), 32) for i in range(4))
            pg.op("dve", lambda e: e.memset(Ec[:, :, 0:1], 1.0), writes=sk(0))
            pg.op("dve", lambda e: e.memset(Es[:, :, 0:1], 0.0), writes=sk(1))
            cp("dve", Rp[:, :, 0], rB, KB, sk(2))
            pg.op("dve", lambda e: e.memset(Mr[:, :, 0:1], 0.0), writes=sk(3))
            cp("dve", Mr[:, :, 1:32], bc_last(rB, 31), KB, sk(3))
            mre, mim, rk = bsm(7), bsm(8), bsm(11)
            cp("dve", mre, cB, KB, KB)
            cp("dve", mim, sB, KB, KB)
            cp("dve", rk, rB, KB, KB)
            t1 = v3(slot(10), 32)
            t2 = v3(slot(11), 32)
            k = 1
            while k < 32:
                cmul("dve", Ec[:, :, k:2 * k], Es[:, :, k:2 * k], Ec[:, :, 0:k], Es[:, :, 0:k],
                     bc_last(mre, k), bc_last(mim, k), t1[:, :, 0:k], t2[:, :, 0:k],
                     sk(0), sk(1), sk(0, 1), KB, sk(10), sk(11))
                tt("dve", Rp[:, :, k:2 * k], Rp[:, :, 0:k], bc_last(rk, k), E.mult, sk(2) + KB, sk(2))
                cmul("dve", bsm(9), bsm(10), mre, mim, mre, mim, bsm(5), bsm(6), KB, KB, KB, KB, KB, KB)
                cp("dve", mre, bsm(9), KB, KB)
                cp("dve", mim, bsm(10), KB, KB)
                tt("dve", rk, rk, rk, E.mult, KB, KB)
                k *= 2
            tt("dve", bsm(12), rk, mre, E.mult, KB, KB)
            tt("dve", bsm(13), rk, mim, E.mult, KB, KB)

        def u_proj(l):
            for b in range(4):
                s_ = proj_block_load(w_u[l, b])
                wv = wr_gu.view(s_)
                for cc in range(2):
                    ct = b * 2 + cc
                    for th in range(2):
                        sl = slice(th * 512, (th + 1) * 512)
                        pb = 6 + (psrr[0] % 2)
                        psrr[0] += 1
                        for kc in range(KC):
                            mm(ps[pb], wv[:, kc * 256 + cc * 128: kc * 256 + cc * 128 + 128], xbf[:, kc, sl],
                               kc == 0, kc == KC - 1, [wr_gu.key(s_), ("xbf", kc, th)], [("ps", pb)])
                        cp("act", u_bf[:, ct, sl], ps[pb], [("ps", pb)], [kU(ct)])

        def s5_pass(l, post):
            E = ALU
            Ec, Es, Rp, Mr = (v3(slot(i), 32) for i in range(4))
            BBr = v3(slot(29, 512, 0).bitcast(BF16), 128)
            BBi = v3(slot(29, 512, 512).bitcast(BF16), 128)
            CCr = v3(slot(30, 512, 0).bitcast(BF16), 32)
            nCCr = v3(slot(30, 512, 512).bitcast(BF16), 32)
            nCCi = v3(slot(31, 512, 0).bitcast(BF16), 32)
            Wre, Wim = v3(slot(4), 32), v3(slot(5), 32)
            rSre, rSim = v3(slot(6), 32), v3(slot(7), 32)
            t1, t2, t3, t4, Rt = slot(10), slot(11), slot(12), slot(13), slot(14)
            p1 = slot(15, 512, 0).bitcast(BF16)
            p2 = slot(15, 512, 512).bitcast(BF16)
            p3 = slot(28, 512, 0).bitcast(BF16)
            p4 = slot(28, 512, 512).bitcast(BF16)
            bre_ps = psall[:, 0:1024]
            bim_ps = psall[:, 1024:2048]
            yps = psall[:, 2048:3072]
            for q in range(32):
                ct, j4 = q // 4, q % 4
                rows = slice(j4 * 32, (j4 + 1) * 32)
                for th in range(2):
                    sl = slice(th * 512, (th + 1) * 512)
                    mm(psall[:, th * 512:(th + 1) * 512], BBr[rows, ct, :], u_bf[rows, ct, sl], True, True,
                       sk(29) + [kU(ct)], [("ps", th)], tile_position=(j4 * 32, 0))
                    mm(psall[:, 1024 + th * 512: 1024 + (th + 1) * 512], BBi[rows, ct, :], u_bf[rows, ct, sl], True, True,
                       sk(29) + [kU(ct)], [("ps", 2 + th)], tile_position=(j4 * 32, 0))
                ecb = bc_mid(Ec[:, q, :], 32)
                esb = bc_mid(Es[:, q, :], 32)
                b3re, b3im = v3(bre_ps, 32), v3(bim_ps, 32)
                PR, PI = [("ps", 0), ("ps", 1)], [("ps", 2), ("ps", 3)]
                tt("dve", v3(t1, 32), b3re, ecb, E.mult, PR + sk(0), sk(10))
                tt("dve", v3(t2, 32), b3im, esb, E.mult, PI + sk(1), sk(11))
                tt("dve", v3(t3, 32), b3im, ecb, E.mult, PI + sk(0), sk(12))
                tt("dve", v3(t4, 32), b3re, esb, E.mult, PR + sk(1), sk(13))
                tt("pool", t1, t1, t2, E.add, sk(10, 11), sk(10))
                tt("pool", t3, t3, t4, E.subtract, sk(12, 13), sk(12))
                if post:
                    tt("pool", v3(t1, 32)[:, :, 0], v3(t1, 32)[:, :, 0], rSre[:, q, :], E.add, sk(10, 6), sk(10))
                    tt("pool", v3(t3, 32)[:, :, 0], v3(t3, 32)[:, :, 0], rSim[:, q, :], E.add, sk(12, 7), sk(12))
                cp("pool", v3(Rt, 32), bc_mid(Mr[:, q, :], 32), sk(3), sk(14))
                pg.op("dve", lambda e: e.tensor_tensor_scan(out=t2, data0=Rt, data1=t1, initial=0.0, op0=E.mult, op1=E.add),
                      reads=sk(14, 10), writes=sk(11))
                pg.op("dve", lambda e: e.tensor_tensor_scan(out=t4, data0=Rt, data1=t3, initial=0.0, op0=E.mult, op1=E.add),
                      reads=sk(14, 12), writes=sk(13))
                if not post:
                    cp("act", Wre[:, q, :], v3(t2, 32)[:, :, 31], sk(11), sk(4))
                    cp("act", Wim[:, q, :], v3(t4, 32)[:, :, 31], sk(13), sk(5))
                else:
                    tt("dve", v3(p1, 32), v3(t2, 32), ecb, E.mult, sk(11, 0), sk(15))
                    tt("pool", v3(p2, 32), v3(t4, 32), esb, E.mult, sk(13, 1), sk(15))
                    tt("dve", v3(p3, 32), v3(t2, 32), esb, E.mult, sk(11, 1), sk(28))
                    tt("pool", v3(p4, 32), v3(t4, 32), ecb, E.mult, sk(13, 0), sk(28))
                    for th in range(2):
                        sl = slice(th * 512, (th + 1) * 512)
                        o_ = yps[rows, sl]
                        for i, (lh, rh, kk) in enumerate(((CCr, p1, sk(30, 15)), (nCCr, p2, sk(30, 15)),
                                                          (nCCi, p3, sk(31, 28)), (nCCi, p4, sk(31, 28)))):
                            mm(o_, lh[:, q, :], rh[:, sl], i == 0, i == 3, kk, [("ps", 4 + th)],
                               tile_position=(0, j4 * 32))
                    if j4 == 3:
                        yf = slot(27)
                        gi = slot(26)
                        dcol = sm(("ssmd", l), 1, ct)
                        stt("dve", yf, u_bf[:, ct, :], dcol, yps, E.mult, E.add,
                            [kU(ct), "small", ("ps", 4), ("ps", 5)], sk(27))
                        tt("pool", gi, yf, yf, E.mult, sk(27), sk(26))
                        ts("pool", gi, gi, 0.044715, 1.0, E.mult, E.add, sk(26), sk(26))
                        tt("pool", gi, gi, yf, E.mult, sk(26, 27), sk(26))
                        act(gi, gi, AF.Sigmoid, sk(26), sk(26), scale=1.5957691216057308)
                        tt("dve", y1[:, ct, :], yf, gi, E.mult, sk(26, 27), [kY1(ct)])

        def s5_state_pre(l):
            E = ALU
            Ec, Es = v3(slot(0), 32), v3(slot(1), 32)
            Wre, Wim = v3(slot(4), 32), v3(slot(5), 32)
            sre, sim = v3(slot(6), 32), v3(slot(7), 32)
            Sre, Sim = v3(slot(8), 32), v3(slot(9), 32)
            t1, t2 = v3(slot(10), 32), v3(slot(11), 32)
            e31c, e31s = bc_last(Ec[:, :, 31], 32), bc_last(Es[:, :, 31], 32)
            cmul("dve", sre, sim, Wre, Wim, e31c, e31s, t1, t2, sk(6), sk(7), sk(4, 5), sk(0, 1), sk(10), sk(11))
            l32r, l32i = bsm(12), bsm(13)
            pg.op("dve", lambda e: e.memset(Sre[:, 0, :], 0.0), writes=sk(8))
            pg.op("dve", lambda e: e.memset(Sim[:, 0, :], 0.0), writes=sk(9))
            fin = slot(15, 64, 64)
            for a in range(32):
                if a < 31:
                    nre, nim = Sre[:, a + 1, :], Sim[:, a + 1, :]
                else:
                    nre, nim = fin[:, 0:32], fin[:, 32:64]
                ta, tb = bsm(14), bsm(15)
                tt("dve", ta, Sre[:, a, :], l32r, E.mult, sk(8) + KB, KB)
                tt("dve", tb, Sim[:, a, :], l32i, E.mult, sk(9) + KB, KB)
                tt("dve", ta, ta, tb, E.subtract, KB, KB)
                tt("dve", nre, ta, sre[:, :, a], E.add, KB + sk(6), sk(8) if a < 31 else sk(15))
                tt("dve", ta, Sre[:, a, :], l32i, E.mult, sk(8) + KB, KB)
                tt("dve", tb, Sim[:, a, :], l32r, E.mult, sk(9) + KB, KB)
                tt("dve", ta, ta, tb, E.add, KB, KB)
                tt("dve", nim, ta, sim[:, :, a], E.add, KB + sk(7), sk(9) if a < 31 else sk(15))
            dma("sp", st_src[:, 0:64], fin, sk(15), ["st_src"])

        def s5_state_post(l):
            E = ALU
            Sre, Sim = v3(slot(8), 32), v3(slot(9), 32)
            Lre, Lim = v3(slot(4), 32), v3(slot(5), 32)
            rSre, rSim = v3(slot(6), 32), v3(slot(7), 32)
            t1, t2 = v3(slot(10), 32), v3(slot(11), 32)
            t3, t4 = v3(slot(12), 32), v3(slot(13), 32)
            sin_ = slot(15, 64, 64)
            dma("sp", sin_, st_all[0:128, 0:64], ["st_all"], sk(15))
            ts("dve", sin_, sin_, sm("flag"), None, E.mult, None, sk(15) + ["small"], sk(15))
            pg.op("dve", lambda e: e.memset(Lre[:, 0:1, :], 1.0), writes=sk(4))
            pg.op("dve", lambda e: e.memset(Lim[:, 0:1, :], 0.0), writes=sk(5))
            mre, mim = bsm(7), bsm(8)
            cp("dve", mre, bsm(12), KB, KB)
            cp("dve", mim, bsm(13), KB, KB)
            k = 1
            while k < 32:
                cmul("dve", Lre[:, k:2 * k, :], Lim[:, k:2 * k, :], Lre[:, 0:k, :], Lim[:, 0:k, :],
                     bc_mid(mre, k), bc_mid(mim, k), t1[:, 0:k, :], t2[:, 0:k, :],
                     sk(4), sk(5), sk(4, 5), KB, sk(10), sk(11))
                cmul("dve", bsm(9), bsm(10), mre, mim, mre, mim, bsm(5), bsm(6), KB, KB, KB, KB, KB, KB)
                cp("dve", mre, bsm(9), KB, KB)
                cp("dve", mim, bsm(10), KB, KB)
                k *= 2
            cmul("dve", t3, t4, Lre, Lim, bc_mid(sin_[:, 0:32], 32), bc_mid(sin_[:, 32:64], 32), t1, t2,
                 sk(12), sk(13), sk(4, 5), sk(15), sk(10), sk(11))
            tt("dve", t3, t3, Sre, E.add, sk(12, 8), sk(12))
            tt("dve", t4, t4, Sim, E.add, sk(13, 9), sk(13))
            cB, sB, rB = bsm(3), bsm(4), bsm(2)
            tt("dve", bsm(9), cB, rB, E.mult, KB, KB)
            tt("dve", bsm(10), sB, rB, E.mult, KB, KB)
            cmul("dve", rSre.rearrange("p q a -> p a q"), rSim.rearrange("p q a -> p a q"), t3, t4,
                 bc_mid(bsm(9), 32), bc_mid(bsm(10), 32), t1, t2, sk(6), sk(7), sk(12, 13), KB, sk(10), sk(11))

        def s5_glu(l):
            for b in range(4):
                s_ = wr_br.load(w_glu[l, b])
                wv = wr_br.view(s_)
                for cc in range(2):
                    co = b * 2 + cc
                    for th in range(2):
                        sl = slice(th * 512, (th + 1) * 512)
                        pb = 6 + (psrr[0] % 2)
                        psrr[0] += 1
                        for kc in range(8):
                            mm(ps[pb], wv[:, kc * 256 + cc * 128: kc * 256 + cc * 128 + 128], y1[:, kc, sl],
                               kc == 0, kc == 7, [wr_br.key(s_), kY1(kc)], [("ps", pb)])
                        gt = slot(10 + (psrr[0] % 2), 512)
                        kg = sk(10 + (psrr[0] % 2))
                        act(gt, ps[pb], AF.Sigmoid, [("ps", pb), "small"], kg, bias=sm(("glub", l), 1, co))
                        tt("dve", y2[:, co, sl], y1[:, co, sl], gt, ALU.mult, [kY1(co)] + kg, [kY2(co)])

        def merge(l):
            E = ALU
            for dc in range(16):
                sg_ = proj_block_load(w_gate[l, dc])
                sb_ = wr_br.load(w_br[l, dc])
                wg = wr_gu.view(sg_)
                wb = wr_br.view(sb_)
                for th in range(2):
                    sl = slice(th * 512, (th + 1) * 512)
                    for g2 in range(2):
                        for kc in range(KC):
                            mm(ps[g2], wg[:, kc * 256 + g2 * 128: kc * 256 + g2 * 128 + 128], xbf[:, kc, sl],
                               kc == 0, kc == KC - 1, [wr_gu.key(sg_), ("xbf", kc, th)], [("ps", g2)])
                    for kc in range(8):
                        mm(ps[2], wb[:, kc * 128:(kc + 1) * 128], y2[:, kc, sl], kc == 0, kc == 7,
                           [wr_br.key(sb_), kY2(kc)], [("ps", 2)])
                    for kc in range(8):
                        mm(ps[3], wb[:, (8 + kc) * 128:(9 + kc) * 128], ygdn[:, kc, sl], kc == 0, kc == 7,
                           [wr_br.key(sb_), kYG(kc)], [("ps", 3)])
                    g0, g1 = slot(0, 512), slot(1, 512)
                    act(g0, ps[0], AF.Sigmoid, [("ps", 0)], sk(0))
                    act(g1, ps[1], AF.Sigmoid, [("ps", 1)], sk(1))
                    tt("dve", g0, g0, ps[2], E.mult, sk(0) + [("ps", 2)], sk(0))
                    tt("dve", g1, g1, ps[3], E.mult, sk(1) + [("ps", 3)], sk(1))
                    tt("pool", merged[:, dc, sl], g0, g1, E.add, sk(0, 1), [("mg", dc, th), kM(dc)])

        def out_proj(l):
            E = ALU
            for b in range(8):
                s_ = proj_block_load(w_o[l, b])
                wv = wr_gu.view(s_)
                for cc in range(2):
                    dc = b * 2 + cc
                    for th in range(2):
                        sl = slice(th * 512, (th + 1) * 512)
                        pb = 6 + (psrr[0] % 2)
                        psrr[0] += 1
                        for kc in range(KC):
                            mm(ps[pb], wv[:, kc * 256 + cc * 128: kc * 256 + cc * 128 + 128], merged[:, kc, sl],
                               kc == 0, kc == KC - 1, [wr_gu.key(s_), ("mg", kc, th)], [("ps", pb)])
                        stt("dve", x32[:, dc, sl], x32[:, dc, sl], float(ALPHA), ps[pb], E.mult, E.add,
                            [("x32", dc, th), ("ps", pb)], [("x32", dc, th)])

        def mixer(l):
            halo_and_spill(l)
            s5_prep(l, True)
            u_proj(l)
            s5_pass(l, False)
            s5_state_pre(l)
            if do_gdn:
                gdn_prep_all(l)
            else:
                pg.op("dve", lambda e: e.memset(slot(14), 0.0), writes=sk(14))
                dma("sp", st_src[:, 64:64 + 1024], slot(14), sk(14), ["st_src"])
            collective(st_src, st_all, ["st_src"], ["st_all"])
            misc_name[0] = f"misc{l}b"
            s5_state_post(l)
            s5_pass(l, True)
            s5_glu(l)
            dump("y2", y2, [128, 8, T], [kY2(c) for c in range(8)])
            if do_gdn:
                gdn_scan2_all(l)
            else:
                for c in range(0, 8, 2):
                    pg.op("pool", lambda e, c=c: e.memset(ygdn[:, c:c + 2, :], 0.0), writes=[kYG(c)])
            merge(l)
            barrier()
            dma("sp", x32flat, xsp_d, ["xsp"], XK)
            out_proj(l)
            barrier()

        gU = nc.dram_tensor("gU", [8, 64, 2048], BF16, kind="Internal").ap()
        gW = nc.dram_tensor("gW", [8, 128, 1024], BF16, kind="Internal").ap()
        gK = nc.dram_tensor("gK", [8, 64, 2048], BF16, kind="Internal").ap()
        gQ = nc.dram_tensor("gQ", [8, 128, 1024], BF16, kind="Internal").ap()
        gA = nc.dram_tensor("gA", [8, 64, 1024], BF16, kind="Internal").ap()
        HD = 128
        QSCALE = float(HD ** -0.5)

        def bcast_row(G, kG, row, nparts, banks):
            Gm = slot(25)[0:64, :]
            ts("dve", Gm, G[0:64, :], ident[0:64, row:row + 1], None, ALU.mult, None, kG + ["consts"], sk(25))
            for th in range(2):
                mm(ps[banks[th]][0:nparts, :], ones[0:64, 0:nparts], Gm[:, th * 512:(th + 1) * 512], True, True,
                   ["ones"] + sk(25), [("ps", banks[th])])

        def gdn_gates(l):
            E = ALU
            G1, G2, G3, CM = slot(26), slot(27), slot(25), slot(24)
            s_ = wr_br.load(w_ba[l], elems=KC * 64)
            wv = wr_br.view(s_)
            for th in range(2):
                sl = slice(th * 512, (th + 1) * 512)
                for kc in range(KC):
                    mm(ps[3 + th][0:64, :], wv[:, kc * 64:(kc + 1) * 64], xbf[:, kc, sl], kc == 0, kc == KC - 1,
                       [wr_br.key(s_), ("xbf", kc, th)], [("ps", 3 + th)])
            bl = psall[0:64, 3 * 512: 5 * 512]
            P34 = [("ps", 3), ("ps", 4)]
            pg.op("dve", lambda e: e.memset(CM[32:64, :], 1.0), writes=sk(24))
            pg.op("dve", lambda e: e.memset(v3(CM[32:64, :], 64)[:, :, 0:1], 0.0), writes=sk(24))
            pg.op("dve", lambda e: e.memset(G2[0:32, :], 0.0), writes=sk(27))
            pg.op("dve", lambda e: e.memset(G3[0:32, :], 0.0), writes=sk(25))
            act(G1[0:32, :], bl[0:32, :], AF.Sigmoid, P34, sk(26))
            gt = G3[32:64, :]
            act(gt, bl[32:64, :], AF.Exp, P34 + ["small"], sk(25), bias=small[32:64, SO[("dtb", l)]:SO[("dtb", l)] + 1])
            act(gt, gt, AF.Ln, sk(25) + ["onec"], sk(25), bias=onec[32:64, 0:1])
            na = bar[32:64, 4:5]
            act(na, small[32:64, SO[("alog", l)]:SO[("alog", l)] + 1], AF.Exp, ["small"], ["na"])
            ts("dve", na, na, -1.0, None, E.mult, None, ["na"], ["na"])
            ts("dve", gt, gt, na, None, E.mult, None, sk(25) + ["na"], sk(25))
            pg.op("dve", lambda e: e.tensor_tensor_scan(out=G1[32:64, :], data0=CM[32:64, :], data1=gt, initial=0.0,
                                                        op0=E.mult, op1=E.add),
                  reads=sk(25, 24), writes=sk(26))
            gc3 = v3(G1[32:64, :], 64)
            act(G2[32:64, :], G1[32:64, :], AF.Exp, sk(26), sk(27))
            tt("dve", v3(G3[32:64, :], 64), bc_last(gc3[:, :, 63], 64), gc3, E.subtract, sk(26), sk(25))
            act(G3[32:64, :], G3[32:64, :], AF.Exp, sk(25), sk(25))
            TM = slot(28, 512).rearrange("p (c q h) -> p c q h", c=16, q=4)
            for gi, (G, kk, cols) in enumerate(((G1, sk(26), ((0, 0), (1, 32))), (G2, sk(27), ((2, 32),)),
                                                (G3, sk(25), ((3, 32),)))):
                for half in range(2):
                    for cc in range(8):
                        c = half * 8 + cc
                        pg.op("pe", lambda e, G=G, c=c, cc=cc: e.transpose(
                            psall[0:64, 5 * 512 + cc * 64: 5 * 512 + (cc + 1) * 64], G[0:64, c * 64:(c + 1) * 64],
                            ident[0:64, 0:64]), reads=kk + ["consts"], writes=[("ps", 5)])
                    src = psall[0:64, 5 * 512: 6 * 512].rearrange("p (c r) -> p c r", r=64)
                    for (qty, c0) in cols:
                        cp("act", TM[0:64, half * 8:(half + 1) * 8, qty, :], src[:, :, c0:c0 + 8], [("ps", 5)], sk(28))

        def gdn_prep(l, h):
            E = ALU
            G1, G2 = slot(26), slot(27)
            TM = slot(28, 512).rearrange("p (c q h) -> p c q h", c=16, q=4)
            EX = x32flat[:, 4 * SW: 4 * SW + 1027]
            Q, K, V = slot(6), slot(7), slot(10)
            sqb = slot(11, 512, 0).bitcast(BF16)
            qhT = slot(11, 512, 512).bitcast(BF16)
            rn = slot(12)
            qdT = slot(13, 512, 0).bitcast(BF16)
            khT = slot(13, 512, 512).bitcast(BF16)
            bv = slot(14)[0:64, :].bitcast(BF16).rearrange("p (c d) -> p c d", d=128)
            bkeg = slot(20)[0:64, :].bitcast(BF16).rearrange("p (c d) -> p c d", d=128)
            kd = slot(21)[0:64, :].bitcast(BF16).rearrange("p (c d) -> p c d", d=128)
            ntok = [(0, 509, 3), (509, 1021, 512), (1021, 1024, 1024)]
            blocks = {}

            def project(which):
                bi = 2 * h + (which // 2)
                if bi not in blocks:
                    blocks[bi] = proj_block_load(w_qkvz[l, bi])
                s_ = blocks[bi]
                wv = wr_gu.view(s_)
                c0 = (which % 2) * 128
                for kc in range(KC):
                    mm(psall[:, 0:3], wv[:, kc * 256 + c0: kc * 256 + c0 + 128], xhbf[:, kc, 0:3], kc == 0, kc == KC - 1,
                       [wr_gu.key(s_), "xhbf"], [("ps", 0)])
                for (t0, t1_, e0) in ntok:
                    for kc in range(KC):
                        mm(psall[:, e0:e0 + (t1_ - t0)], wv[:, kc * 256 + c0: kc * 256 + c0 + 128], xbf[:, kc, t0:t1_],
                           kc == 0, kc == KC - 1, [wr_gu.key(s_)] + XBK[2 * kc: 2 * kc + 2], [("ps", e0 // 512)])
                cp("act", EX, psall[:, 0:1027], [("ps", 0), ("ps", 1), ("ps", 2)], sk(4, 5))

            def conv_silu(dst, kdst, tile, eng):
                wb = SO[("convw", l)] + tile * 4
                ts(eng, dst, EX[:, 0:1024], small[:, wb:wb + 1], None, E.mult, None, sk(4, 5) + ["small"], kdst)
                for i in range(1, 4):
                    if eng == "dve":
                        stt(eng, dst, EX[:, i:i + 1024], small[:, wb + i:wb + i + 1], dst, E.mult, E.add,
                            sk(4, 5) + ["small"] + kdst, kdst)
                    else:
                        ts(eng, rn, EX[:, i:i + 1024], small[:, wb + i:wb + i + 1], None, E.mult, None,
                           sk(4, 5) + ["small"], sk(12))
                        tt(eng, dst, dst, rn, E.add, kdst + sk(12), kdst)
                act(dst, dst, AF.Silu, kdst, kdst)

            def rnorm(src, ksrc):
                act(sqb, src, AF.Square, ksrc, sk(11))
                for th in range(2):
                    mm(ps[3 + th], onesb[:], sqb[:, th * 512:(th + 1) * 512], True, True, ["onesb"] + sk(11), [("ps", 3 + th)])
                act(rn, psall[:, 3 * 512:5 * 512], AF.Sqrt, [("ps", 3), ("ps", 4), "eps6"], sk(12), bias=eps6[:, 0:1])
                pg.op("dve", lambda e: e.reciprocal(out=rn, in_=rn), reads=sk(12), writes=sk(12))

            project(0)
            conv_silu(Q, sk(6), 0 * 8 + h, "dve")
            project(1)
            conv_silu(K, sk(7), 1 * 8 + h, "pool")
            project(2)
            conv_silu(V, sk(10), 2 * 8 + h, "pool")
            rnorm(Q, sk(6))
            stt("dve", Q, Q, QSCALE, rn, E.mult, E.mult, sk(6, 12), sk(6))
            cp("act", qhT, Q, sk(6), sk(11))
            bcast_row(G2, sk(27), 32 + h, 128, (5, 6))
            egc = psall[:, 5 * 512: 7 * 512]
            P56 = [("ps", 5), ("ps", 6)]
            tt("dve", qdT, Q, egc, E.mult, sk(6) + P56, sk(13))
            gl = glt[:].rearrange("p (h c) -> p h c", c=16)
            cp("act", gl[:, h, :], v3(egc, 64)[:, :, 63], P56, ["glt"])
            rnorm(K, sk(7))
            tt("dve", K, K, rn, E.mult, sk(7, 12), sk(7))
            cp("act", khT, K, sk(7), sk(13))
            betac = TM[0:64, :, 0, h]
            egcc = TM[0:64, :, 2, h]
            kdfc = TM[0:64, :, 3, h]
            gcol = TM[0:64, :, 1, h]
            beg = slot(12, 16, 0)[0:64, :]
            tt("dve", beg, betac, egcc, E.mult, sk(28), sk(12))
            for half in range(2):
                for cc in range(8):
                    c = half * 8 + cc
                    pg.op("pe", lambda e, c=c, cc=cc: e.transpose(
                        psall[0:64, cc * 128:(cc + 1) * 128], K[:, c * 64:(c + 1) * 64], ident[:]),
                        reads=sk(7) + ["consts"], writes=[("ps", cc // 4)])
                kt = psall[0:64, 0:1024].rearrange("p (c d) -> p c d", d=128)
                hs = slice(half * 8, (half + 1) * 8)
                tt("dve", bkeg[:, hs, :], kt, bc_last(beg[:, hs], 128), E.mult, [("ps", 0), ("ps", 1)] + sk(12), sk(20))
                tt("dve", kd[:, hs, :], kt, bc_last(kdfc[:, hs], 128), E.mult, [("ps", 0), ("ps", 1)] + sk(28), sk(21))
                for cc in range(8):
                    c = half * 8 + cc
                    pg.op("pe", lambda e, c=c, cc=cc: e.transpose(
                        psall[0:64, 1024 + cc * 128: 1024 + (cc + 1) * 128], V[:, c * 64:(c + 1) * 64], ident[:]),
                        reads=sk(10) + ["consts"], writes=[("ps", 2 + cc // 4)])
                vt = psall[0:64, 1024:2048].rearrange("p (c d) -> p c d", d=128)
                tt("dve", bv[:, hs, :], vt, bc_last(betac[:, hs], 128), E.mult, [("ps", 2), ("ps", 3)] + sk(28), sk(14))
            bcast_row(G1, sk(26), h, 64, (4, 5))
            bcast_row(G1, sk(26), 32 + h, 64, (6, 7))
            brow = v3(psall[0:64, 4 * 512:6 * 512], 64)
            grow = v3(psall[0:64, 6 * 512:8 * 512], 64)
            P45, P67 = [("ps", 4), ("ps", 5)], [("ps", 6), ("ps", 7)]
            d1 = v3(slot(22)[0:64, :], 64)
            d2 = v3(slot(23)[0:64, :], 64)
            d3 = v3(slot(24)[0:64, :], 64)
            tt("dve", d1, bc_last(gcol, 64), grow, E.subtract, sk(28) + P67, sk(22))
            ts("dve", d2, d1, -1.0, None, E.mult, None, sk(22), sk(23))
            tt("pool", d3, d2, bc_mid(maskIU, 16), E.add, sk(23) + ["consts"], sk(24))
            tt("pool", d1, d1, bc_mid(maskSL, 16), E.add, sk(22) + ["consts"], sk(22))
            tt("pool", d2, d2, bc_mid(maskSU, 16), E.add, sk(23) + ["consts"], sk(23))
            for dd, kk in ((d1, sk(22)), (d2, sk(23)), (d3, sk(24))):
                act(dd, dd, AF.Exp, kk, kk)
            for c in range(16):
                cs = slice(c * 64, (c + 1) * 64)
                mm(psall[0:64, cs], khT[:, cs], khT[:, cs], True, True, sk(13), [("ps", c // 8)])
                mm(psall[0:64, 1024 + c * 64: 1024 + (c + 1) * 64], khT[:, cs], qhT[:, cs], True, True, sk(13, 11),
                   [("ps", 2 + c // 8)])
            akk = v3(psall[0:64, 0:1024], 64)
            aqk = v3(psall[0:64, 1024:2048], 64)
            P01, P23 = [("ps", 0), ("ps", 1)], [("ps", 2), ("ps", 3)]
            L_, LT = d1, d2
            tt("dve", L_, L_, akk, E.mult, sk(22) + P01, sk(22))
            tt("dve", L_, L_, bc_last(betac, 64), E.mult, sk(22, 28), sk(22))
            tt("dve", LT, LT, akk, E.mult, sk(23) + P01, sk(23))
            tt("dve", LT, LT, brow, E.mult, sk(23) + P45, sk(23))
            AT = slot(5, 512, 512)[0:64, :].bitcast(BF16)
            tt("dve", v3(AT, 64), d3, aqk, E.mult, sk(24) + P23, sk(5))
            TT = v3(slot(4)[0:64, :], 64)
            tt("dve", TT, bc_mid(ident[0:64, 0:64], 16), LT, E.subtract, ["consts"] + sk(23), sk(4))
            Pc, PTc, kP, kPT = L_, LT, sk(22), sk(23)
            Pn, PTn, kPn, kPTn = d3, v3(slot(25)[0:64, :], 64), sk(24), sk(25)
            for lev in range(5):
                for c in range(16):
                    mm(psall[0:64, c * 64:(c + 1) * 64], PTc[:, c, :], Pc[:, c, :], True, True, kP + kPT, [("ps", c // 8)])
                    if lev < 4:
                        mm(psall[0:64, 1024 + c * 64: 1024 + (c + 1) * 64], Pc[:, c, :], PTc[:, c, :], True, True,
                           kP + kPT, [("ps", 2 + c // 8)])
                cp("act", Pn, v3(psall[0:64, 0:1024], 64), P01, kPn)
                if lev < 4:
                    cp("act", PTn, v3(psall[0:64, 1024:2048], 64), P23, kPTn)
                for c in range(16):
                    mm(psall[0:64, 2048 + c * 64: 2048 + (c + 1) * 64], Pn[:, c, :], TT[:, c, :], True, True,
                       kPn + sk(4), [("ps", 4 + c // 8)])
                tt("dve", TT, TT, v3(psall[0:64, 2048:3072], 64), E.add, sk(4) + P45, sk(4))
                Pc, Pn, kP, kPn = Pn, Pc, kPn, kP
                PTc, PTn, kPT, kPTn = PTn, PTc, kPTn, kPT
            TTb = slot(5, 512, 0)[0:64, :].bitcast(BF16)
            cp("act", v3(TTb, 64), TT, sk(4), sk(5))
            TTb3 = v3(TTb, 64)
            Ub = slot(6)[0:64, :].bitcast(BF16)
            for c in range(16):
                mm(psall[0:64, c * 128:(c + 1) * 128], TTb3[:, c, :], bv[:, c, :], True, True, sk(5, 14), [("ps", c // 4)])
                mm(psall[:, 2048 + c * 64: 2048 + (c + 1) * 64], bkeg[:, c, :], TTb3[:, c, :], True, True, sk(20, 5),
                   [("ps", 4 + c // 8)])
            cp("act", Ub, psall[0:64, 0:2048], [("ps", i) for i in range(4)], sk(6))
            WT = slot(12, 512, 512).bitcast(BF16)
            cp("dve", WT, psall[:, 2048:3072], P45, sk(12))
            dma("sp", gU[h], Ub, sk(6), [("gU", h)])
            dma("sp", gW[h], WT, sk(12), [("gW", h)])
            dma("sp", gK[h], slot(21)[0:64, :].bitcast(BF16), sk(21), [("gK", h)])
            dma("sp", gQ[h], qdT, sk(13), [("gQ", h)])
            dma("sp", gA[h], AT, sk(5), [("gA", h)])

        def gdn_scan(l, pass2):
            E = ALU
            gl = glt[:].rearrange("p (h c) -> p h c", c=16)
            for grp in range(2):
                heads = [grp * 4 + i for i in range(4)]
                bufs = {}
                base = list(range(0, 20)) if pass2 else [4, 5, 6, 7, 10, 11, 12, 13, 14, 20, 21, 22]
                it = iter(base)
                for h in heads:
                    sU, sWK = next(it), next(it)
                    Ub = slot(sU)[0:64, :].bitcast(BF16).rearrange("p (c d) -> p c d", d=128)
                    WT = slot(sWK, 512, 0).bitcast(BF16)
                    dma("sp", slot(sU)[0:64, :].bitcast(BF16), gU[h], [("gU", h)], sk(sU))
                    dma("sp", WT, gW[h], [("gW", h)], sk(sWK))
                    sK = next(it)
                    kdv = slot(sK)[0:64, :].bitcast(BF16).rearrange("p (c d) -> p c d", d=128)
                    dma("sp", slot(sK)[0:64, :].bitcast(BF16), gK[h], [("gK", h)], sk(sK))
                    d = dict(U=Ub, kU=sk(sU), WT=WT, kW=sk(sWK), kd=kdv, kK=sk(sK))
                    if pass2:
                        sQA, sO = next(it), next(it)
                        d["qd"] = slot(sQA, 512, 0).bitcast(BF16)
                        d["AT"] = slot(sQA, 512, 512)[0:64, :].bitcast(BF16)
                        dma("sp", d["qd"], gQ[h], [("gQ", h)], sk(sQA))
                        dma("sp", d["AT"], gA[h], [("gA", h)], sk(sQA))
                        d["kQA"] = sk(sQA)
                        d["O"] = slot(sO)
                        d["kO"] = sk(sO)
                    d["S"] = gst[:, (h % 4) * 128:(h % 4 + 1) * 128]
                    d["Sb"] = gstb[:, (h % 4) * 128:(h % 4 + 1) * 128]
                    d["vn"] = gvn[0:64, (h % 4) * 128:(h % 4 + 1) * 128]
                    d["kS"], d["kSb"], d["kvn"] = [("gS", h % 4)], [("gSb", h % 4)], [("gvn", h % 4)]
                    if pass2:
                        dma("sp", d["S"], st_all[0:128, 64 + h * 128: 64 + (h + 1) * 128], ["st_all"], d["kS"])
                        ts("dve", d["S"], d["S"], sm("flag"), None, E.mult, None, d["kS"] + ["small"], d["kS"])
                    else:
                        pg.op("dve", lambda e, S=d["S"]: e.memset(S, 0.0), writes=d["kS"])
                    cp("act", d["Sb"], d["S"], d["kS"], d["kSb"])
                    bufs[h] = d
                for c in range(16):
                    cs = slice(c * 64, (c + 1) * 64)
                    for i, h in enumerate(heads):
                        d = bufs[h]
                        pv = psall[0:64, i * 128:(i + 1) * 128]
                        pS = psall[:, 512 + i * 128: 512 + (i + 1) * 128]
                        po = psall[:, 1024 + i * 64: 1024 + (i + 1) * 64]
                        mm(pv, d["WT"][:, cs], d["Sb"], True, True, d["kW"] + d["kSb"], [("psg", 0, i)])
                        tt("dve", d["vn"], d["U"][:, c, :], pv, E.subtract, d["kU"] + [("psg", 0, i)], d["kvn"])
                        if pass2:
                            mm(po, d["Sb"], d["qd"][:, cs], True, False, d["kSb"] + d["kQA"], [("psg", 2, i)])
                            mm(po, d["vn"], d["AT"][:, cs], False, True, d["kvn"] + d["kQA"], [("psg", 2, i)])
                            cp("act", d["O"][:, cs], po, [("psg", 2, i)], d["kO"])
                        mm(pS, d["kd"][:, c, :], d["vn"], True, True, d["kK"] + d["kvn"], [("psg", 1, i)])
                        stt("dve", d["S"], d["S"], gl[:, h, c:c + 1], pS, E.mult, E.add,
                            d["kS"] + ["glt"] + [("psg", 1, i)], d["kS"])
                        cp("act", d["Sb"], d["S"], d["kS"], d["kSb"])
                for i, h in enumerate(heads):
                    d = bufs[h]
                    if not pass2:
                        dma("sp", st_src[:, 64 + h * 128: 64 + (h + 1) * 128], d["S"], d["kS"], ["st_src"])
                    else:
                        gdn_out(l, h, d)

        def gdn_out(l, h, d):
            E = ALU
            O = d["O"]
            kO = d["kO"]
            sqb = slot(20, 512, 0).bitcast(BF16)
            rs = slot(21)
            zs = slot(22)
            act(sqb, O, AF.Square, kO, sk(20))
            for th in range(2):
                mm(ps[3 + th], onesb[:], sqb[:, th * 512:(th + 1) * 512], True, True, ["onesb"] + sk(20), [("ps", 3 + th)])
            act(rs, psall[:, 3 * 512:5 * 512], AF.Sqrt, [("ps", 3), ("ps", 4), "eps6"], sk(21), bias=eps6[:, 0:1],
                scale=1.0 / HD)
            pg.op("dve", lambda e: e.reciprocal(out=rs, in_=rs), reads=sk(21), writes=sk(21))
            tt("dve", O, O, rs, E.mult, kO + sk(21), kO)
            s_ = proj_block_load(w_qkvz[l, 2 * h + 1])
            wv = wr_gu.view(s_)
            for th in range(2):
                sl = slice(th * 512, (th + 1) * 512)
                for kc in range(KC):
                    mm(ps[5 + th], wv[:, kc * 256 + 128: kc * 256 + 256], xbf[:, kc, sl], kc == 0, kc == KC - 1,
                       [wr_gu.key(s_), ("xbf", kc, th)], [("ps", 5 + th)])
            act(zs, psall[:, 5 * 512:7 * 512], AF.Silu, [("ps", 5), ("ps", 6)], sk(22))
            stt("dve", ygdn[:, h, :], O, sm(("normw", l)), zs, E.mult, E.mult, kO + sk(22) + ["small"], [kYG(h)])

        def gdn_prep_all(l):
            gdn_gates(l)
            for h in range(8):
                gdn_prep(l, h)
            gdn_scan(l, False)

        def gdn_scan2_all(l):
            gdn_scan(l, True)

        for l in range(L):
            if l > 0:
                for rg in (wr_gu, wr_dn, wr_br):
                    rg.new_epoch()
            for part in ("a", "b"):
                pg.new_dma_sem(f"misc{l}{part}")
                pg.shared_sems.add(f"misc{l}{part}")
            misc_name[0] = f"misc{l}a"
            ffn(l, 0)
            layer_norm(l, 0)
            if do_mixer:
                mixer(l)
                layer_norm(l, 1)
            ffn(l, 1)
            layer_norm(l, 2)

        yv = yout.rearrange("(kc p) t -> p kc t", p=128)
        fin = []
        for kc in range(KC):
            ev = dma("sp", yv[:, kc, :], x32[:, kc, :], [("x32", kc, 0), ("x32", kc, 1)], [], sem="out")
            fin = [ev]
        pg.emit(fin)
    return nc


def tile_cols(w, c0, ncols, bn):
    L = w.shape[0]
    a = w[:, :, c0:c0 + ncols].reshape(L, KC, 128, ncols // bn, bn)
    return np.ascontiguousarray(a.transpose(0, 3, 2, 1, 4)).reshape(L, ncols // bn, 128, KC * bn)


def tile_gu(w):
    L = w.shape[0]
    g = w[:, :, :D_FF].reshape(L, KC, 128, NGU, 128)
    u = w[:, :, D_FF:].reshape(L, KC, 128, NGU, 128)
    gu = np.stack([g, u], axis=4).transpose(0, 3, 2, 1, 4, 5)
    return np.ascontiguousarray(gu).reshape(L, NGU, 128, KC * 256)


def tile_dn(w):
    L = w.shape[0]
    a = w.reshape(L, NST, FCS, 128, NDB, 256).transpose(0, 1, 4, 3, 2, 5)
    return np.ascontiguousarray(a).reshape(L, NST, NDB, 128, FCS * 256)


def prep_shared(inp, L):
    f = lambda k: np.asarray(inp[k], np.float32)[:L]
    w_in = f("w_in")
    sh = {
        "wgu0": tile_gu(f("ffn1_w_gu")), "wgu1": tile_gu(f("ffn2_w_gu")),
        "wdn0": tile_dn(f("ffn1_w_down")), "wdn1": tile_dn(f("ffn2_w_down")),
        "w_u": tile_cols(w_in, IN_U0, 1024, 256),
        "w_o": tile_cols(f("w_out"), 0, 2048, 256),
    }
    q = w_in[:, :, IN_Q0:IN_Q0 + 1024].reshape(L, KC, 128, 8, 128)
    k = w_in[:, :, IN_K0:IN_K0 + 1024].reshape(L, KC, 128, 8, 128)
    v = w_in[:, :, IN_V0:IN_V0 + 1024].reshape(L, KC, 128, 8, 128)
    z = w_in[:, :, IN_Z0:IN_Z0 + 1024].reshape(L, KC, 128, 8, 128)
    qkvz = np.stack([np.stack([q, k], axis=4), np.stack([v, z], axis=4)], axis=4)
    qkvz = qkvz.transpose(0, 3, 4, 2, 1, 5, 6)
    sh["w_qkvz"] = np.ascontiguousarray(qkvz).reshape(L, 16, 128, KC * 256)
    ba = np.zeros((L, KC, 128, 64), np.float32)
    ba[..., 0:8] = w_in[:, :, IN_B0:IN_B0 + 8].reshape(L, KC, 128, 8)
    ba[..., 32:40] = w_in[:, :, IN_A0:IN_A0 + 8].reshape(L, KC, 128, 8)
    sh["w_ba"] = np.ascontiguousarray(ba.transpose(0, 2, 1, 3)).reshape(L, 128, KC * 64)
    gs = w_in[:, :, IN_GS0:IN_GS0 + 2048].reshape(L, KC, 128, 16, 128)
    gg = w_in[:, :, IN_GG0:IN_GG0 + 2048].reshape(L, KC, 128, 16, 128)
    gate = np.stack([gs, gg], axis=4).transpose(0, 3, 2, 1, 4, 5)
    sh["w_gate"] = np.ascontiguousarray(gate).reshape(L, 16, 128, KC * 256)
    brs = f("w_br_ssm").reshape(L, 8, 128, 16, 128)
    brg = f("w_br_gdn").reshape(L, 8, 128, 16, 128)
    br = np.concatenate([brs, brg], axis=1).transpose(0, 3, 2, 1, 4)
    sh["w_br"] = np.ascontiguousarray(br).reshape(L, 16, 128, 2048)
    gl = f("glu_w").reshape(L, 8, 128, 4, 256).transpose(0, 3, 2, 1, 4)
    sh["w_glu"] = np.ascontiguousarray(gl).reshape(L, 4, 128, 2048)
    are, aim, ldt = f("ssm_a_re"), f("ssm_a_im"), f("ssm_log_dt")
    bre, bim = f("ssm_b_re"), f("ssm_b_im")
    def lay_a(x_gp):
        a = x_gp.reshape(L, 8, 8, 64)
        a = np.broadcast_to(a[:, :, :, None, :], (L, 8, 8, 16, 64)).transpose(0, 2, 3, 1, 4)
        return np.ascontiguousarray(a).reshape(L, 128, 512)
    def lay_ab(x_gph):
        a = x_gph.reshape(L, 8, 8, 64, 16).transpose(0, 2, 4, 1, 3)
        return np.ascontiguousarray(a).reshape(L, 128, 512)
    ldt_gp = np.broadcast_to(ldt[:, :, None], (L, 64, 64))
    sh["s5a"] = np.ascontiguousarray(np.concatenate(
        [lay_a(are), lay_a(aim), lay_a(ldt_gp), lay_ab(bre), lay_ab(bim)], axis=2))
    cre, cim = f("ssm_c_re"), f("ssm_c_im")
    def lay_c(c):
        out = np.zeros((L, 2, 64, 32, 2, 16), np.float32)
        cc = c.reshape(L, 32, 2, 16, 64)
        for g2 in range(2):
            out[:, g2, :, :, g2, :] = cc[:, :, g2, :, :].transpose(0, 3, 1, 2)
        return out.reshape(L, 128, 1024)
    sh["s5c"] = np.ascontiguousarray(np.concatenate([lay_c(cre), lay_c(cim)], axis=2))
    SO = small_layout(L)
    sm = np.zeros((128, SO["_n"]), np.float32)
    sm[0::1, SO["maskE"]] = ((np.arange(128) // 16) % 2 == 0)
    sm[:, SO["maskO"]] = ((np.arange(128) // 16) % 2 == 1)
    sm[:, SO["negpi"]] = -np.pi
    lng = [f("ln1_g"), f("ln2_g"), f("ln3_g")]
    lnb = [f("ln1_b"), f("ln2_b"), f("ln3_b")]
    convw = f("conv_w")
    for l in range(L):
        o = SO[("ln", l)]
        for i in range(3):
            sm[:, o + i * 2 * KC: o + i * 2 * KC + KC] = lng[i][l].reshape(KC, 128).T
            sm[:, o + i * 2 * KC + KC: o + (i + 1) * 2 * KC] = lnb[i][l].reshape(KC, 128).T
        cw = convw[l].reshape(4, 24, 128).transpose(2, 1, 0)
        sm[:, SO[("convw", l)]: SO[("convw", l)] + 96] = cw.reshape(128, 96)
        sm[:, SO[("glub", l)]: SO[("glub", l)] + 8] = f("glu_b")[l].reshape(8, 128).T
        sm[:, SO[("ssmd", l)]: SO[("ssmd", l)] + 8] = f("ssm_d")[l].reshape(8, 128).T
        sm[:, SO[("normw", l)]] = f("gdn_norm_w")[l]
        sm[32:40, SO[("alog", l)]] = f("gdn_a_log")[l]
        sm[32:40, SO[("dtb", l)]] = f("gdn_dt_bias")[l]
        def lay_b(x_gp):
            return x_gp.reshape(32, 2, 64).transpose(1, 2, 0).reshape(128, 32)
        o = SO[("s5b", l)]
        sm[:, o:o + 32] = lay_b(are[l])
        sm[:, o + 32:o + 64] = lay_b(aim[l])
        sm[:, o + 64:o + 96] = lay_b(ldt_gp[l])
    cst = np.zeros((128, 320), np.float32)
    cst[:, 0:128] = np.eye(128, dtype=np.float32)
    i = np.arange(64)[:, None]
    j = np.arange(64)[None, :]
    NEG = -1e30
    cst[0:64, 128:192] = np.where(i > j, 0.0, NEG)
    cst[0:64, 192:256] = np.where(j > i, 0.0, NEG)
    cst[0:64, 256:320] = np.where(j >= i, 0.0, NEG)
    sh["consts"] = cst
    return sh, sm, SO


N_LAYERS_PER_LAUNCH = 1


def kernel(**inp):
    x = np.asarray(inp["x"], np.float32)
    LL = N_LAYERS_PER_LAUNCH
    nc = build_program(LL)
    wkeys = [k for k in inp if k != "x"]
    cur = [np.ascontiguousarray(x[c // 2, (c % 2) * T:(c % 2 + 1) * T, :].T) for c in range(NCORES)]
    for l0 in range(0, DEPTH, LL):
        sub = {k: np.asarray(inp[k])[l0:l0 + LL] for k in wkeys}
        sh, sm, SO = prep_shared(sub, LL)
        in_maps = []
        for c in range(NCORES):
            m = dict(sh)
            s = sm.copy()
            s[:, SO["flag"]] = float(c % 2)
            m["small"] = s
            m["xT"] = cur[c]
            in_maps.append(m)
        res = run_bass_kernel_spmd(nc, in_maps, core_ids=list(range(NCORES)))
        cur = [np.ascontiguousarray(res.results[c]["yT"]) for c in range(NCORES)]
        del sh, in_maps
    out = np.empty((BATCH, SEQ, D_MODEL), np.float32)
    for c in range(NCORES):
        b, h = c // 2, c % 2
        out[b, h * T:(h + 1) * T, :] = cur[c].T
    return out
```
